# Optimizing a Trainium2 kernel written in Bass

```python
import math
import jax, jax.numpy as jnp
from jax import lax
import numpy as np

D_MODEL = 1024
BATCH = 16
SEQ = 2048
DEPTH = 4

CTX_LEN = 256
GRID_W = 64
HEAD_DIM = 64
Q_BLOCK = 128
ROPE_THETA = 10000.0
ROPE_HALF = HEAD_DIM // 4
ATTN_SCALE = HEAD_DIM ** -0.5
A_HEADS = 8
A_KV_HEADS = 2
A_GROUP = A_HEADS // A_KV_HEADS
A_WIDTH = A_HEADS * HEAD_DIM
B_WIDTH = D_MODEL // 2
CONV_W = 3
C_HEADS = 4
C_WIDTH = C_HEADS * 2 * HEAD_DIM
SUBLN_DIM = 2 * HEAD_DIM
N_BRANCH = 3
EPS = 1e-6
PROJ_WIDTHS = (
    A_WIDTH, A_KV_HEADS * HEAD_DIM, A_KV_HEADS * HEAD_DIM, A_WIDTH,
    B_WIDTH, B_WIDTH, B_WIDTH, B_WIDTH,
    C_WIDTH, C_WIDTH, C_WIDTH, C_WIDTH,
    N_BRANCH * D_MODEL,
)
PROJ_TOTAL = sum(PROJ_WIDTHS)

kernel_name = "hybrid_gqa_shortconv_diffattn_prefix_dit"


def rmsnorm(x, g=None):
    xf = x.astype(jnp.float32)
    y = xf * lax.rsqrt(jnp.mean(xf * xf, axis=-1, keepdims=True) + EPS)
    if g is not None:
        y = y * g.astype(jnp.float32)
    return y.astype(x.dtype)


def split_proj(p):
    idx = np.cumsum(PROJ_WIDTHS)[:-1].tolist()
    return jnp.split(p, idx, axis=-1)


def axial_rope(x, cos, sin):
    shp = x.shape
    xr = x.astype(jnp.float32).reshape(*shp[:-1], 2, 2, ROPE_HALF)
    x1, x2 = xr[..., 0, :], xr[..., 1, :]
    bshape = (shp[1],) + (1,) * (x.ndim - 3) + (2, ROPE_HALF)
    cs, sn = cos.reshape(bshape), sin.reshape(bshape)
    out = jnp.stack([x1 * cs - x2 * sn, x2 * cs + x1 * sn], axis=-2)
    return out.reshape(shp).astype(x.dtype)


def sweep_query_blocks(fn, q):
    b, s = q.shape[:2]
    nb = s // Q_BLOCK
    qb = jnp.moveaxis(q.reshape(b, nb, Q_BLOCK, *q.shape[2:]), 1, 0)
    out = lax.map(fn, qb)
    return jnp.moveaxis(out, 0, 1).reshape(b, s, *out.shape[3:])


def gqa(q, k, v):
    s = jnp.einsum('bqkgd,bskd->bkgqs', q, k).astype(jnp.float32) * ATTN_SCALE
    p = jax.nn.softmax(s, axis=-1)
    return jnp.einsum('bkgqs,bskd->bqkgd', p.astype(v.dtype), v)


def diff_attention(q, k, v, lam):
    s = jnp.einsum('bqhmd,bshmd->bhmqs', q, k).astype(jnp.float32) * ATTN_SCALE
    p = jax.nn.softmax(s, axis=-1)
    a = p[:, :, 0] - lam * p[:, :, 1]
    return jnp.einsum('bhqs,bshe->bqhe', a.astype(v.dtype), v)


def short_conv(z, w):
    zp = jnp.pad(z, ((0, 0), (1, 1), (0, 0)))
    return w[0] * zp[:, :-2] + w[1] * zp[:, 1:-1] + w[2] * zp[:, 2:]


def merge_branches(ya, yb, yc, ag, bg, cg, mg, wa, wb, wc, wo):
    ga, gb, gc = jnp.split(jax.nn.sigmoid(mg), N_BRANCH, axis=-1)
    merged = (ga * ((ya * jax.nn.silu(ag)) @ wa)
              + gb * ((yb * jax.nn.silu(bg)) @ wb)
              + gc * ((yc * jax.nn.silu(cg)) @ wc))
    return merged @ wo


def setup_inputs(seed: int = 0) -> dict:
    key = jax.random.key(seed)
    ks = jax.random.split(key, 20)
    f32 = jnp.float32
    nrm = lambda k, shp, s: (jax.random.normal(k, shp, f32) * s).astype(f32)
    return {
        "x": nrm(ks[0], (BATCH, SEQ, D_MODEL), 1.0),
        "c": nrm(ks[1], (BATCH, D_MODEL), 1.0),
        "ctx": nrm(ks[2], (BATCH, CTX_LEN, D_MODEL), 1.0),
        "c_ctx": nrm(ks[3], (D_MODEL,), 1.0),
        "w_mod": nrm(ks[4], (DEPTH, D_MODEL, 3 * D_MODEL), 0.5 * D_MODEL ** -0.5),
        "b_mod": nrm(ks[5], (DEPTH, 3 * D_MODEL), 0.02),
        "w_in": nrm(ks[6], (DEPTH, D_MODEL, PROJ_TOTAL), D_MODEL ** -0.5),
        "q_norm_a": 1.0 + nrm(ks[7], (DEPTH, HEAD_DIM), 0.02),
        "k_norm_a": 1.0 + nrm(ks[8], (DEPTH, HEAD_DIM), 0.02),
        "conv_w_b": nrm(ks[9], (DEPTH, CONV_W, B_WIDTH), CONV_W ** -0.5),
        "lam_q1": nrm(ks[10], (DEPTH, HEAD_DIM), 0.1),
        "lam_k1": nrm(ks[11], (DEPTH, HEAD_DIM), 0.1),
        "lam_q2": nrm(ks[12], (DEPTH, HEAD_DIM), 0.1),
        "lam_k2": nrm(ks[13], (DEPTH, HEAD_DIM), 0.1),
        "subln_c": 1.0 + nrm(ks[14], (DEPTH, SUBLN_DIM), 0.02),
        "w_branch_a": nrm(ks[15], (DEPTH, A_WIDTH, D_MODEL), A_WIDTH ** -0.5),
        "w_branch_b": nrm(ks[16], (DEPTH, B_WIDTH, D_MODEL), B_WIDTH ** -0.5),
        "w_branch_c": nrm(ks[17], (DEPTH, C_WIDTH, D_MODEL), C_WIDTH ** -0.5),
        "w_out": nrm(ks[18], (DEPTH, D_MODEL, D_MODEL), D_MODEL ** -0.5),
        "final_norm": 1.0 + nrm(ks[19], (D_MODEL,), 0.02),
    }


def reference(x, c, ctx, c_ctx, w_mod, b_mod, w_in, q_norm_a, k_norm_a, conv_w_b,
              lam_q1, lam_k1, lam_q2, lam_k2, subln_c, w_branch_a, w_branch_b,
              w_branch_c, w_out, final_norm):
    B, S, D = x.shape
    L = ctx.shape[1]
    ROWS = S // GRID_W
    rows = jnp.repeat(jnp.arange(ROWS), GRID_W).astype(jnp.float32)
    cols = jnp.tile(jnp.arange(GRID_W), ROWS).astype(jnp.float32)
    inv_freq = ROPE_THETA ** (-jnp.arange(ROPE_HALF, dtype=jnp.float32) / ROPE_HALF)
    ang = jnp.stack([rows[:, None] * inv_freq, cols[:, None] * inv_freq], axis=1)
    cos, sin = jnp.cos(ang), jnp.sin(ang)

    for i in range(DEPTH):
        last = i == DEPTH - 1
        lambda_init = 0.8 - 0.6 * math.exp(-0.3 * i)
        shift, scale, gate = jnp.split(jax.nn.silu(c) @ w_mod[i] + b_mod[i], 3, axis=-1)
        shift_c, scale_c, gate_c = jnp.split(jax.nn.silu(c_ctx) @ w_mod[i] + b_mod[i], 3, axis=-1)
        hx = rmsnorm(x) * (1 + scale[:, None]) + shift[:, None]
        hc = rmsnorm(ctx) * (1 + scale_c) + shift_c
        (aq, ak, av, ag, bh, bb, bc, bg, cq, ck, cv, cg, mg) = split_proj(hx @ w_in[i])
        (aq_c, ak_c, av_c, ag_c, bh_c, bb_c, bc_c, bg_c,
         cq_c, ck_c, cv_c, cg_c, mg_c) = split_proj(hc @ w_in[i])

        qa = axial_rope(rmsnorm(aq.reshape(B, S, A_HEADS, HEAD_DIM), q_norm_a[i]), cos, sin)
        ka = axial_rope(rmsnorm(ak.reshape(B, S, A_KV_HEADS, HEAD_DIM), k_norm_a[i]), cos, sin)
        va = av.reshape(B, S, A_KV_HEADS, HEAD_DIM)
        ka_c = rmsnorm(ak_c.reshape(B, L, A_KV_HEADS, HEAD_DIM), k_norm_a[i])
        va_c = av_c.reshape(B, L, A_KV_HEADS, HEAD_DIM)
        ka_all = jnp.concatenate([ka_c, ka], axis=1)
        va_all = jnp.concatenate([va_c, va], axis=1)
        ya = sweep_query_blocks(lambda qb: gqa(qb, ka_all, va_all),
                                qa.reshape(B, S, A_KV_HEADS, A_GROUP, HEAD_DIM)).reshape(B, S, A_WIDTH)

        yb = bb * short_conv(bc * bh, conv_w_b[i])

        lam = (jnp.exp(jnp.sum(lam_q1[i].astype(jnp.float32) * lam_k1[i].astype(jnp.float32)))
               - jnp.exp(jnp.sum(lam_q2[i].astype(jnp.float32) * lam_k2[i].astype(jnp.float32)))
               + lambda_init)
        qc = axial_rope(cq.reshape(B, S, C_HEADS, 2, HEAD_DIM), cos, sin)
        kc = axial_rope(ck.reshape(B, S, C_HEADS, 2, HEAD_DIM), cos, sin)
        vc = cv.reshape(B, S, C_HEADS, SUBLN_DIM)
        kc_c = ck_c.reshape(B, L, C_HEADS, 2, HEAD_DIM)
        vc_c = cv_c.reshape(B, L, C_HEADS, SUBLN_DIM)
        kc_all = jnp.concatenate([kc_c, kc], axis=1)
        vc_all = jnp.concatenate([vc_c, vc], axis=1)
        yc = sweep_query_blocks(lambda qb: diff_attention(qb, kc_all, vc_all, lam), qc)
        yc = (rmsnorm(yc, subln_c[i]) * (1 - lambda_init)).reshape(B, S, C_WIDTH)

        out = merge_branches(ya, yb, yc, ag, bg, cg, mg,
                             w_branch_a[i], w_branch_b[i], w_branch_c[i], w_out[i])

        if not last:
            qa_c = rmsnorm(aq_c.reshape(B, L, A_HEADS, HEAD_DIM), q_norm_a[i])
            ya_c = gqa(qa_c.reshape(B, L, A_KV_HEADS, A_GROUP, HEAD_DIM), ka_c, va_c).reshape(B, L, A_WIDTH)
            yb_c = bb_c * short_conv(bc_c * bh_c, conv_w_b[i])
            yc_c = diff_attention(cq_c.reshape(B, L, C_HEADS, 2, HEAD_DIM), kc_c, vc_c, lam)
            yc_c = (rmsnorm(yc_c, subln_c[i]) * (1 - lambda_init)).reshape(B, L, C_WIDTH)
            out_c = merge_branches(ya_c, yb_c, yc_c, ag_c, bg_c, cg_c, mg_c,
                                   w_branch_a[i], w_branch_b[i], w_branch_c[i], w_out[i])
            ctx = ctx + gate_c * out_c

        x = x + gate[:, None] * out

    return rmsnorm(x, final_norm)
```

```python
import math
from contextlib import ExitStack

import numpy as np
import concourse.bass as bass
import concourse.mybir as mybir
from concourse.bass_utils import run_bass_kernel_spmd

F32 = mybir.dt.float32
BF16 = mybir.dt.bfloat16
AF = mybir.ActivationFunctionType
ALU = mybir.AluOpType

D = 1024
S = 2048
L = 256
T = S + L
DEPTH = 4
NCORES = 8
PT = 8448
EPS = 1e-6
SCALE = 64 ** -0.5
NSLOT = 3
C_AQ, C_AK, C_AV, C_AG = 0, 512, 640, 768
C_BH, C_BB, C_BC, C_BG = 1280, 1792, 2304, 2816
C_CQ, C_CK, C_CV, C_CG = 3328, 3840, 4352, 4864
C_MG = 5376
R_C, R_BMOD, R_CONV, R_SUBLN, R_FN, R_QN, R_KN, R_LAM, R_TOT = 0, 24, 120, 168, 172, 180, 184, 188, 204


class Sem:
    def __init__(self, h, name):
        self.h = h
        self.v = 0
        self.name = name


class Prog:
    ENGS = ("pe", "act", "dve", "pool", "sp")

    def __init__(self, nc, stack):
        self.nc = nc
        self.stack = stack
        self.streams = {e: [] for e in self.ENGS}
        self.esem = {e: self.new_sem("e_" + e) for e in self.ENGS if e != "sp"}
        self.waited = {}
        self.lastw = {}
        self.readers = {}

    def new_sem(self, name):
        return Sem(self.stack.enter_context(self.nc.semaphore(name)), name)

    def _deps(self, eng, reads, writes):
        need = {}

        def add(tok, war=False):
            if tok is None:
                return
            s, v, teng = tok
            if teng == eng and eng == "pe":
                return
            if need.get(s, 0) < v:
                need[s] = v

        for k in reads:
            add(self.lastw.get(k))
        for k in writes:
            add(self.lastw.get(k))
            for t in self.readers.get(k, ()):
                add(t, war=True)
        out = []
        for s, v in need.items():
            if self.waited.get((eng, s), 0) < v:
                self.waited[(eng, s)] = v
                out.append((s, v))
        return out

    def _record(self, tok, reads, writes):
        for k in writes:
            self.lastw[k] = tok
            self.readers[k] = []
        for k in reads:
            self.readers.setdefault(k, []).append(tok)

    def op(self, eng, emit, reads=(), writes=(), signal=True):
        waits = self._deps(eng, reads, writes)
        s = self.esem[eng]
        if signal:
            s.v += 1
            tok = (s, s.v, eng)
            self.streams[eng].append((waits, emit, (s, 1)))
        else:
            tok = (s, s.v + 1, eng)
            self.streams[eng].append((waits, emit, None))
        self._record(tok, reads, writes)

    def dma(self, eng, emit, dsem, reads=(), writes=()):
        waits = self._deps(eng, reads, writes)
        if dsem.v > 0 and self.waited.get((eng, dsem), 0) < dsem.v:
            self.waited[(eng, dsem)] = dsem.v
            waits.append((dsem, dsem.v))
        dsem.v += 16
        tok = (dsem, dsem.v, "dma")
        self.streams[eng].append((waits, emit, (dsem, 16)))
        self._record(tok, reads, writes)

    def wait_all(self, eng, keys):
        waits = self._deps(eng, (), keys)
        self.streams[eng].append((waits, None, None))

    def replay(self, eng, e):
        for waits, emit, inc in self.streams[eng]:
            for s, v in waits:
                e.wait_ge(s.h, v)
            if emit is None:
                continue
            ins = emit(e)
            if inc is not None:
                ins.then_inc(inc[0].h, inc[1])

    def run(self):
        with self.nc.Block() as block:
            @block.tensor
            def _(e):
                self.replay("pe", e)

            @block.scalar
            def _(e):
                self.replay("act", e)

            @block.vector
            def _(e):
                self.replay("dve", e)

            @block.gpsimd
            def _(e):
                self.replay("pool", e)

            @block.sync
            def _(e):
                self.replay("sp", e)


def build_program(nl=DEPTH, nb=2, dbg=False):
    nc = bass.Bass("TRN2", target_bir_lowering=False)
    dt = lambda name, shape, dtype, kind: nc.dram_tensor(name, shape, dtype, kind=kind).ap()
    x2 = dt("x2", [nb, S, D], F32, "ExternalInput")
    ctx2 = dt("ctx2", [nb, L, D], F32, "ExternalInput")
    prm = dt("prm", [R_TOT, 128], F32, "ExternalInput")
    cst = dt("cst", [6, 128, 128], F32, "ExternalInput")
    ropec = dt("ropec", [128, S], F32, "ExternalInput")
    ropes = dt("ropes", [128, S], F32, "ExternalInput")
    w_mod = dt("w_mod", [nl, D, 3 * D], F32, "ExternalInput")
    w_in = dt("w_in", [nl, D, PT], F32, "ExternalInput")
    w_a = dt("w_a", [nl, 512, D], F32, "ExternalInput")
    w_b = dt("w_b", [nl, 512, D], F32, "ExternalInput")
    w_c = dt("w_c", [nl, 512, D], F32, "ExternalInput")
    w_o = dt("w_o", [nl, D, D], F32, "ExternalInput")
    out2 = dt("out2", [nb, S, D], F32, "ExternalOutput")
    dbg_t = {}

    with ExitStack() as st:
        P = Prog(nc, st)
        sb = lambda name, shape, dtype: st.enter_context(nc.sbuf_tensor(name, shape, dtype))
        xT = sb("xT", [128, 8, T], F32)
        kaT = sb("kaT", [128, T], BF16)
        va = sb("va", [128, 18, 2, 128], BF16)
        kcT = sb("kcT", [128, 4, T], BF16)
        vc = sb("vc", [128, 18, 512], BF16)
        uT = sb("uT", [128, 4, T + 4], BF16)
        hx = sb("hx", [128, 8, 512], BF16)
        mq = sb("mq", [128, 8, 512], BF16)
        yg = sb("yg", [128, 3, 4, 512], BF16)
        pT = sb("pT", [128, 4, 512], BF16)
        tmp = sb("tmp", [128, 6, 512], F32)
        sq = sb("sq", [128, 2, 512], BF16)
        cosb = sb("cosb", [128, 512], F32)
        sinb = sb("sinb", [128, 512], F32)
        W = sb("W", [128, NSLOT, 8, 256], BF16)
        cF = sb("cF", [128, 2, 128], F32)
        cB = sb("cB", [128, 4, 128], BF16)
        prmA = sb("prmA", [128, 128], F32)
        prmB = sb("prmB", [128, 128], F32)
        prmT = sb("prmT", [128, R_TOT], F32)
        scT = sb("scT", [128, 24], BF16)
        modT = sb("modT", [128, nl, 3, 24], F32)
        ops1 = sb("ops1", [128, nl, 3, 8], F32)
        lamw = sb("lamw", [128, 16], F32)
        nlam = sb("nlam", [128, 4], F32)
        subS = sb("subS", [128, 4], F32)
        epsT = sb("epsT", [128, 1], F32)
        ps = [st.enter_context(nc.psum_tensor(f"ps{i}", [128, 512], F32)) for i in range(8)]

        identF = cF[:, 0, :]
        permF = cF[:, 1, :]
        onesM, ones64, ones128, ones1 = cB[:, 0, :], cB[:, 1, :], cB[:, 2, :], cB[:, 3, :]

        def MM(out, lhsT, rhs, start, stop, reads, writes, signal=None):
            P.op("pe", lambda e: e.matmul(out, lhsT, rhs, start=start, stop=stop), reads=reads, writes=writes,
                 signal=stop if signal is None else signal)

        def TR(out, in_, reads, writes):
            n_ = in_.shape[0]
            P.op("pe", lambda e: e.transpose(out, in_, identF[0:n_, 0:n_]), reads=list(reads) + ["cF"], writes=writes)

        def ACT(out, in_, func, reads, writes, bias=None, scale=1.0):
            if bias is None:
                P.op("act", lambda e: e.activation(out=out, in_=in_, func=func, scale=scale), reads=reads, writes=writes)
            else:
                P.op("act", lambda e: e.activation(out=out, in_=in_, func=func, bias=bias, scale=scale), reads=reads, writes=writes)

        def TT(out, a, b, op, reads, writes, eng="dve"):
            P.op(eng, lambda e: e.tensor_tensor(out, a, b, op), reads=reads, writes=writes)

        def TS(out, in0, s1, s2, op0, op1, reads, writes):
            P.op("dve", lambda e: e.tensor_scalar(out, in0, s1, s2, op0, op1), reads=reads, writes=writes)

        def STT(out, in0, scalar, in1, op0, op1, reads, writes):
            P.op("dve", lambda e: e.scalar_tensor_tensor(out=out, in0=in0, scalar=scalar, in1=in1, op0=op0, op1=op1),
                 reads=reads, writes=writes)

        def RECIP(out, in_, reads, writes):
            P.op("dve", lambda e: e.reciprocal(out, in_), reads=reads, writes=writes)

        def CPY(eng, out, in_, reads, writes):
            if eng == "act":
                ACT(out, in_, AF.Copy, reads, writes)
            else:
                P.op("dve", lambda e: e.tensor_copy(out, in_), reads=reads, writes=writes)

        def MEMSET(out, val, writes):
            P.op("dve", lambda e: e.memset(out, val), writes=writes)

        gsem = [P.new_sem(f"g{i}") for i in range(8)]
        gctr = [0]

        def SPDMA(out, in_, reads, writes):
            s = gsem[gctr[0] % len(gsem)]
            gctr[0] += 1
            P.dma("sp", lambda e: e.dma_start(out=out, in_=in_), s, reads=reads, writes=writes)

        rr = {"proj": 0, "score": 0, "acc": 0}
        pools = {"proj": [0, 1, 2, 3, 4, 5, 6], "score": [0, 1, 2], "acc": [3, 4, 5, 6]}

        def bank(pool):
            b = pools[pool][rr[pool] % len(pools[pool])]
            rr[pool] += 1
            return b

        PK = lambda b: ("ps", b)
        ptc = [0]

        def next_pt():
            i = ptc[0] % 4
            ptc[0] += 1
            return i

        tmc = [0]

        wsem = [[P.new_sem(f"w{s}_{i}") for i in range(9)] for s in range(NSLOT)]
        wctr = [0]

        def wkeys(slot):
            return [("w", slot, i) for i in range(9)]

        def wload(pieces):
            slot = wctr[0] % NSLOT
            wctr[0] += 1
            assert len(pieces) <= 9
            for i, (dstf, src) in enumerate(pieces):
                dst = dstf(slot)
                P.dma("pool", (lambda e, dst=dst, src=src: e.dma_start(out=dst, in_=src)), wsem[slot][i],
                      reads=(), writes=[("w", slot, i)])
            return slot

        def wload_full(pieces):
            slot = wctr[0] % NSLOT
            wctr[0] += 1
            n = len(pieces)
            assert n <= 9
            for i, (dstf, src) in enumerate(pieces):
                dst = dstf(slot)
                wr = [("w", slot, i)]
                if i == 0:
                    wr += [("w", slot, j) for j in range(n, 9)]
                P.dma("pool", (lambda e, dst=dst, src=src: e.dma_start(out=dst, in_=src)), wsem[slot][i],
                      reads=(), writes=wr)
            return slot

        def slab_cols(wmat, col0, ncols=256, nk=8, koff=0):
            src = wmat[:, col0:col0 + ncols].rearrange("(k p) c -> p k c", p=128)
            return (lambda slot: W[:, slot, koff:koff + nk, 0:ncols], src)

        def slab_pairmajor_cols(wmat, base, half):
            pcs = []
            for jj in range(2):
                j = 2 * half + jj
                for hp in range(2):
                    c0 = base + hp * 256 + j * 64
                    src = wmat[:, c0:c0 + 64].rearrange("(k p) c -> p k c", p=128)
                    d0 = jj * 128 + hp * 64
                    pcs.append((lambda slot, d0=d0: W[:, slot, :, d0:d0 + 64], src))
            return pcs

        SPDMA(cF[:], cst[0:2].rearrange("a p c -> p a c"), [], ["cF"])
        for i in range(4):
            P.dma("pool", (lambda e, i=i: e.dma_start(out=cB[:, i, :], in_=cst[2 + i])), wsem[0][i], writes=[("cB", i)])
        CB = [("cB", i) for i in range(4)]
        SPDMA(prmA[:], prm[0:128, :], [], ["prmA"])
        SPDMA(prmB[0:R_TOT - 128, :], prm[128:R_TOT, :], [], ["prmB"])
        MEMSET(epsT[:], EPS, ["epsT"])
        MEMSET(uT[:], 0.0, ["uT"])
        MEMSET(va[:, :, :, 64:128], 1.0, ["va1"])
        TR(ps[0][:, 0:128], prmA[:], ["prmA"], [PK(0)])
        TR(ps[1][:, 0:R_TOT - 128], prmB[0:R_TOT - 128, :], ["prmB"], [PK(1)])
        CPY("dve", prmT[:, 0:128], ps[0][:, 0:128], [PK(0)], ["prmT"])
        CPY("dve", prmT[:, 128:R_TOT], ps[1][:, 0:R_TOT - 128], [PK(1)], ["prmT"])
        ACT(scT[:], prmT[:, 0:24], AF.Silu, ["prmT"], ["scT"])
        for l in range(nl):
            pb = bank("proj")
            for sl in range(12):
                slot = wload_full([slab_cols(w_mod[l], sl * 256)])
                for cc in range(2):
                    jc = sl * 2 + cc
                    for k in range(8):
                        MM(ps[pb][:, jc * 3:jc * 3 + 3], W[:, slot, k, cc * 128:(cc + 1) * 128], scT[:, k:24:8],
                           k == 0, k == 7, wkeys(slot) + ["scT"], [PK(pb)])
            for r in range(3):
                TT(modT[:, l, r, :], ps[pb][:, r:72:3], prmT[:, R_BMOD + l * 24:R_BMOD + (l + 1) * 24], ALU.add,
                   [PK(pb), "prmT"], ["modT"])
                TS(ops1[:, l, r, :], modT[:, l, r, 8:16], 1.0, 0.0, ALU.add, ALU.add, ["modT"], ["ops1"])
        TT(lamw[:, 0:4], prmT[:, R_LAM:R_LAM + 4], prmT[:, R_LAM + 4:R_LAM + 8], ALU.mult, ["prmT"], ["lamw"])
        TT(lamw[:, 4:8], prmT[:, R_LAM + 8:R_LAM + 12], prmT[:, R_LAM + 12:R_LAM + 16], ALU.mult, ["prmT"], ["lamw"])
        MEMSET(tmp[:, 0, 0:128], 1.0, [("tmp", 0)])
        MM(ps[7][:, 0:8], tmp[:, 0, 0:128], lamw[:, 0:8], True, True, [("tmp", 0), "lamw"], [PK(7)])
        ACT(lamw[:, 8:16], ps[7][:, 0:8], AF.Exp, [PK(7)], ["lamw2"])
        TT(nlam[:], lamw[:, 12:16], lamw[:, 8:12], ALU.subtract, ["lamw2"], ["nlam"])
        for l in range(4):
            li = 0.8 - 0.6 * math.exp(-0.3 * l)
            TS(nlam[:, l:l + 1], nlam[:, l:l + 1], -li, 0.0, ALU.add, ALU.add, ["nlam"], ["nlam"])
            TS(subS[:, l:l + 1], prmT[:, R_SUBLN + l:R_SUBLN + l + 1], 1.0 - li, 0.0, ALU.mult, ALU.add, ["prmT"], ["subS"])

        XK = lambda c, bidx: ("x", c, bidx)
        blocks = [(0, 0, L, True)] + [(1 + j, L + 512 * j, 512, False) for j in range(4)]

        def ucol(t0, isctx):
            return 1 + t0 if isctx else 3 + t0

        def load_rope(t0, NT):
            SPDMA(cosb[:, :NT], ropec[:, t0 - L:t0 - L + NT], [], ["cosb"])
            SPDMA(sinb[:, :NT], ropes[:, t0 - L:t0 - L + NT], [], ["sinb"])

        def hx_block(l, r, bidx, t0, NT):
            for c in range(8):
                ACT(sq[:, c % 2, :NT], xT[:, c, t0:t0 + NT], AF.Square, [XK(c, bidx)], [("sq", c % 2)])
                MM(ps[7][:, :NT], onesM, sq[:, c % 2, :NT], c == 0, c == 7, [("sq", c % 2)] + CB, [PK(7)], signal=True)
            ACT(tmp[:, 5, :NT], ps[7][:, :NT], AF.Sqrt, [PK(7), "epsT"], [("tmp", 5)], bias=epsT[:, 0:1])
            RECIP(tmp[:, 5, :NT], tmp[:, 5, :NT], [("tmp", 5)], [("tmp", 5)])
            for c in range(8):
                ti = c % 2
                STT(tmp[:, ti, :NT], xT[:, c, t0:t0 + NT], ops1[:, l, r, c:c + 1], tmp[:, 5, :NT], ALU.mult, ALU.mult,
                    [XK(c, bidx), "ops1", ("tmp", 5)], [("tmp", ti)])
                ACT(hx[:, c, :NT], tmp[:, ti, :NT], AF.Identity, [("tmp", ti), "modT"], [("hx", c)],
                    bias=modT[:, l, r, c:c + 1])

        HXK = [("hx", c) for c in range(8)]

        def proj_fm(slot, coff, NT, pb):
            for k in range(8):
                MM(ps[pb][:, :NT], W[:, slot, k, coff:coff + 128], hx[:, k, :NT], k == 0, k == 7,
                   wkeys(slot) + HXK, [PK(pb)])

        def rope_tail(src_f32, src_key, dst, dst_key, NT, rstd=None):
            pr = bank("proj")
            MM(ps[pr][:, :NT], permF, src_f32, True, True, [src_key, "cF"], [PK(pr)])
            TT(tmp[:, 2, :NT], src_f32, cosb[:, :NT], ALU.mult, [src_key, "cosb"], [("tmp", 2)])
            TT(tmp[:, 3, :NT], ps[pr][:, :NT], sinb[:, :NT], ALU.mult, [PK(pr), "sinb"], [("tmp", 3)])
            if rstd is None:
                TT(dst, tmp[:, 2, :NT], tmp[:, 3, :NT], ALU.add, [("tmp", 2), ("tmp", 3)], [dst_key])
            else:
                TT(tmp[:, 2, :NT], tmp[:, 2, :NT], tmp[:, 3, :NT], ALU.add, [("tmp", 2), ("tmp", 3)], [("tmp", 2)])
                TT(dst, tmp[:, 2, :NT], rstd, ALU.mult, [("tmp", 2), ("tmp", 4)], [dst_key])

        def post_a(pb, gcol, dst, dst_key, NT, isctx):
            ACT(sq[:, 0, :NT], ps[pb][:, :NT], AF.Square, [PK(pb)], [("sq", 0)])
            ACT(tmp[:, 0, :NT], ps[pb][:, :NT], AF.Identity, [PK(pb), "prmT"], [("tmp", 0)], scale=prmT[:, gcol:gcol + 1])
            MM(ps[7][:, :NT], ones64, sq[:, 0, :NT], True, True, [("sq", 0)] + CB, [PK(7)])
            ACT(tmp[:, 4, :NT], ps[7][:, :NT], AF.Sqrt, [PK(7), "epsT"], [("tmp", 4)], bias=epsT[:, 0:1])
            RECIP(tmp[:, 4, :NT], tmp[:, 4, :NT], [("tmp", 4)], [("tmp", 4)])
            if isctx:
                TT(dst, tmp[:, 0, :NT], tmp[:, 4, :NT], ALU.mult, [("tmp", 0), ("tmp", 4)], [dst_key])
            else:
                rope_tail(tmp[:, 0, :NT], ("tmp", 0), dst, dst_key, NT, rstd=tmp[:, 4, :NT])

        def post_c(pb, dst, dst_key, NT, isctx):
            if isctx:
                CPY("act", dst, ps[pb][:, :NT], [PK(pb)], [dst_key])
            else:
                CPY("act", tmp[:, 0, :NT], ps[pb][:, :NT], [PK(pb)], [("tmp", 0)])
                rope_tail(tmp[:, 0, :NT], ("tmp", 0), dst, dst_key, NT)

        def phase1(l, bi, blk):
            bidx, t0, NT, isctx = blk
            r = 2 if isctx else bi
            wi = w_in[l]
            if not isctx:
                load_rope(t0, NT)
            hx_block(l, r, bidx, t0, NT)
            ntile = NT // 128
            slot = wload_full([slab_cols(wi, C_AK)])
            pb = bank("proj")
            proj_fm(slot, 0, NT, pb)
            post_a(pb, R_KN + l, kaT[:, t0:t0 + NT], ("kaT", bidx), NT, isctx)
            for tt in range(ntile):
                pb = bank("proj")
                for k in range(8):
                    MM(ps[pb][:, 0:128], hx[:, k, tt * 128:(tt + 1) * 128], W[:, slot, k, 128:256], k == 0, k == 7,
                       wkeys(slot) + HXK, [PK(pb)])
                gt = t0 // 128 + tt
                for g in range(2):
                    CPY("act" if g == 0 else "dve", va[:, gt, g, 0:64], ps[pb][:, g * 64:(g + 1) * 64], [PK(pb)],
                        [("va", gt, g)])
            for sl in range(2):
                slot = wload_full([slab_cols(wi, C_CK + sl * 256)])
                for cc in range(2):
                    h = sl * 2 + cc
                    pb = bank("proj")
                    proj_fm(slot, cc * 128, NT, pb)
                    post_c(pb, kcT[:, h, t0:t0 + NT], ("kcT", h, bidx), NT, isctx)
            s0 = wload_full([slab_cols(wi, C_CV)])
            s1 = wload_full([slab_cols(wi, C_CV + 256)])
            for tt in range(ntile):
                pb = bank("proj")
                for hf, slot in enumerate((s0, s1)):
                    for k in range(8):
                        MM(ps[pb][:, hf * 256:(hf + 1) * 256], hx[:, k, tt * 128:(tt + 1) * 128], W[:, slot, k, :],
                           k == 0, k == 7, wkeys(slot) + HXK, [PK(pb)], signal=(k == 7))
                gt = t0 // 128 + tt
                CPY("act" if tt % 2 == 0 else "dve", vc[:, gt, :], ps[pb][:, :], [PK(pb)], [("vc", gt)])
            uc = ucol(t0, isctx)
            for sl in range(2):
                sh = wload_full([slab_cols(wi, C_BH + sl * 256)])
                sc_ = wload_full([slab_cols(wi, C_BC + sl * 256)])
                for cc in range(2):
                    c = sl * 2 + cc
                    p1 = bank("proj")
                    proj_fm(sh, cc * 128, NT, p1)
                    p2 = bank("proj")
                    proj_fm(sc_, cc * 128, NT, p2)
                    CPY("act", tmp[:, 1, :NT], ps[p1][:, :NT], [PK(p1)], [("tmp", 1)])
                    TT(uT[:, c, uc:uc + NT], ps[p2][:, :NT], tmp[:, 1, :NT], ALU.mult, [PK(p2), ("tmp", 1)], [("uT", c, bidx), "uT"])

        def attn_unit(k_ap_fn, q_ap, q_keys, k_keys_fn, pv_list, NT, tiles):
            pend = None
            n = len(tiles)
            for i in range(n + 1):
                if i < n:
                    stt = tiles[i]
                    sbk = bank("score")
                    MM(ps[sbk][:, :NT], k_ap_fn(stt), q_ap, True, True, list(q_keys) + list(k_keys_fn(stt)), [PK(sbk)])
                    pi = next_pt()
                    ACT(pT[:, pi, :NT], ps[sbk][:, :NT], AF.Exp, [PK(sbk)], [("pT", pi)], scale=SCALE)
                    cur = (stt, pi, i)
                if pend is not None:
                    pst, ppi, pidx = pend
                    for (ab, lf, kf) in pv_list:
                        MM(ps[ab][:, :NT], lf(pst), pT[:, ppi, :NT], pidx == 0, pidx == n - 1,
                           [("pT", ppi)] + list(kf(pst)), [PK(ab)])
                pend = cur if i < n else None

        def phase2(l, bi, blk, last):
            bidx, t0, NT, isctx = blk
            r = 2 if isctx else bi
            wi = w_in[l]
            tiles = [0, 1] if isctx else list(range(18))
            tblk = lambda tl: 0 if tl < 2 else 1 + (tl - 2) // 4
            if not isctx:
                load_rope(t0, NT)
            hx_block(l, r, bidx, t0, NT)
            for half in range(2):
                slot = wload_full(slab_pairmajor_cols(wi, C_AQ, half))
                for jj in range(2):
                    j = 2 * half + jj
                    pb = bank("proj")
                    proj_fm(slot, jj * 128, NT, pb)
                    post_a(pb, R_QN + l, mq[:, j, :NT], ("mq", j), NT, isctx)
            for half in range(2):
                slot = wload_full(slab_pairmajor_cols(wi, C_AG, half))
                for jj in range(2):
                    j = 2 * half + jj
                    pb = bank("proj")
                    proj_fm(slot, jj * 128, NT, pb)
                    ACT(tmp[:, 1, :NT], ps[pb][:, :NT], AF.Silu, [PK(pb)], [("tmp", 1)])
                    for hh in range(2):
                        ab = bank("acc")
                        lo, hi = 64 * hh, 64 * hh + 64
                        attn_unit(lambda tl, lo=lo, hi=hi: kaT[lo:hi, tl * 128:(tl + 1) * 128],
                                  mq[lo:hi, j, :NT], [("mq", j)],
                                  lambda tl: [("kaT", tblk(tl))],
                                  [(ab, (lambda tl, hh=hh: va[:, tl, hh, :]), (lambda tl, hh=hh: [("va", tl, hh), "va1"]))],
                                  NT, tiles)
                        RECIP(tmp[0:64, 2, :NT], ps[ab][64:128, :NT], [PK(ab)], [("tmp", 2)])
                        TT(tmp[lo:hi, 3, :NT], ps[ab][0:64, :NT], tmp[0:64, 2, :NT], ALU.mult, [PK(ab), ("tmp", 2)], [("tmp", 3)])
                        TT(yg[lo:hi, 0, j, :NT], tmp[lo:hi, 3, :NT], tmp[lo:hi, 1, :NT], ALU.mult,
                           [("tmp", 3), ("tmp", 1)], [("yg", 0, j, hh)])
            for sl in range(2):
                slot = wload_full([slab_cols(wi, C_CQ + sl * 256)])
                for cc in range(2):
                    h = sl * 2 + cc
                    pb = bank("proj")
                    proj_fm(slot, cc * 128, NT, pb)
                    post_c(pb, mq[:, h, :NT], ("mq", h), NT, isctx)
            for sl in range(2):
                slot = wload_full([slab_cols(wi, C_CG + sl * 256)])
                for cc in range(2):
                    h = sl * 2 + cc
                    pb = bank("proj")
                    proj_fm(slot, cc * 128, NT, pb)
                    ACT(tmp[:, 1, :NT], ps[pb][:, :NT], AF.Silu, [PK(pb)], [("tmp", 1)])
                    accs = [bank("acc") for _ in range(4)]
                    for m in range(2):
                        lo, hi = 64 * m, 64 * m + 64
                        attn_unit(lambda tl, lo=lo, hi=hi, h=h: kcT[lo:hi, h, tl * 128:(tl + 1) * 128],
                                  mq[lo:hi, h, :NT], [("mq", h)],
                                  lambda tl, h=h: [("kcT", h, tblk(tl))],
                                  [(accs[2 * m], (lambda tl, h=h: vc[:, tl, h * 128:(h + 1) * 128]), (lambda tl: [("vc", tl)])),
                                   (accs[2 * m + 1], (lambda tl: ones1), (lambda tl: CB))],
                                  NT, tiles)
                    RECIP(tmp[:, 2, :NT], ps[accs[1]][:, :NT], [PK(accs[1])], [("tmp", 2)])
                    TT(tmp[:, 2, :NT], ps[accs[0]][:, :NT], tmp[:, 2, :NT], ALU.mult, [PK(accs[0]), ("tmp", 2)], [("tmp", 2)])
                    RECIP(tmp[:, 3, :NT], ps[accs[3]][:, :NT], [PK(accs[3])], [("tmp", 3)])
                    TT(tmp[:, 3, :NT], ps[accs[2]][:, :NT], tmp[:, 3, :NT], ALU.mult, [PK(accs[2]), ("tmp", 3)], [("tmp", 3)])
                    STT(tmp[:, 2, :NT], tmp[:, 3, :NT], nlam[:, l:l + 1], tmp[:, 2, :NT], ALU.mult, ALU.add,
                        [("tmp", 2), ("tmp", 3), "nlam"], [("tmp", 2)])
                    ACT(sq[:, 1, :NT], tmp[:, 2, :NT], AF.Square, [("tmp", 2)], [("sq", 1)])
                    MM(ps[7][:, :NT], ones128, sq[:, 1, :NT], True, True, [("sq", 1)] + CB, [PK(7)])
                    ACT(tmp[:, 4, :NT], ps[7][:, :NT], AF.Sqrt, [PK(7), "epsT"], [("tmp", 4)], bias=epsT[:, 0:1])
                    RECIP(tmp[:, 4, :NT], tmp[:, 4, :NT], [("tmp", 4)], [("tmp", 4)])
                    STT(tmp[:, 2, :NT], tmp[:, 2, :NT], subS[:, l:l + 1], tmp[:, 4, :NT], ALU.mult, ALU.mult,
                        [("tmp", 2), ("tmp", 4), "subS"], [("tmp", 2)])
                    TT(yg[:, 2, h, :NT], tmp[:, 2, :NT], tmp[:, 1, :NT], ALU.mult, [("tmp", 2), ("tmp", 1)], [("yg", 2, h)])
            uc = ucol(t0, isctx)
            for sl in range(2):
                sbb = wload_full([slab_cols(wi, C_BB + sl * 256)])
                sbg = wload_full([slab_cols(wi, C_BG + sl * 256)])
                for cc in range(2):
                    c = sl * 2 + cc
                    p1 = bank("proj")
                    proj_fm(sbb, cc * 128, NT, p1)
                    p2 = bank("proj")
                    proj_fm(sbg, cc * 128, NT, p2)
                    ACT(tmp[:, 1, :NT], ps[p2][:, :NT], AF.Silu, [PK(p2)], [("tmp", 1)])
                    cw = lambda tap: prmT[:, R_CONV + l * 12 + tap * 4 + c:R_CONV + l * 12 + tap * 4 + c + 1]
                    TS(tmp[:, 2, :NT], uT[:, c, uc - 1:uc - 1 + NT], cw(0), 0.0, ALU.mult, ALU.add, ["uT", "prmT"], [("tmp", 2)])
                    STT(tmp[:, 3, :NT], uT[:, c, uc:uc + NT], cw(1), tmp[:, 2, :NT], ALU.mult, ALU.add,
                        ["uT", "prmT", ("tmp", 2)], [("tmp", 3)])
                    STT(tmp[:, 2, :NT], uT[:, c, uc + 1:uc + 1 + NT], cw(2), tmp[:, 3, :NT], ALU.mult, ALU.add,
                        ["uT", "prmT", ("tmp", 3)], [("tmp", 2)])
                    TT(tmp[:, 3, :NT], ps[p1][:, :NT], tmp[:, 2, :NT], ALU.mult, [PK(p1), ("tmp", 2)], [("tmp", 3)])
                    TT(yg[:, 1, c, :NT], tmp[:, 3, :NT], tmp[:, 1, :NT], ALU.mult, [("tmp", 3), ("tmp", 1)], [("yg", 1, c)])
            YA = [("yg", 0, j, hh) for j in range(4) for hh in range(2)]
            YB = [("yg", 1, c) for c in range(4)]
            YC = [("yg", 2, c) for c in range(4)]
            for cp in range(4):
                pcs = []
                for k in range(4):
                    for hp in range(2):
                        r0 = hp * 256 + k * 64
                        pcs.append((lambda slot, k=k, hp=hp: W[hp * 64:hp * 64 + 64, slot, k, :],
                                    w_a[l][r0:r0 + 64, cp * 256:(cp + 1) * 256]))
                pcs.append(slab_cols(w_b[l], cp * 256, nk=4, koff=4))
                sga = wload_full([slab_cols(wi, C_MG + cp * 256)])
                sab = wload_full(pcs)
                for cc in range(2):
                    c = cp * 2 + cc
                    pg = bank("proj")
                    proj_fm(sga, cc * 128, NT, pg)
                    ACT(tmp[:, 1, :NT], ps[pg][:, :NT], AF.Sigmoid, [PK(pg)], [("tmp", 1)])
                    py = bank("proj")
                    for k in range(4):
                        MM(ps[py][:, :NT], W[:, sab, k, cc * 128:(cc + 1) * 128], yg[:, 0, k, :NT], k == 0, k == 3,
                           wkeys(sab) + YA, [PK(py)])
                    TT(tmp[:, 2 + cc, :NT], ps[py][:, :NT], tmp[:, 1, :NT], ALU.mult, [PK(py), ("tmp", 1)], [("tmp", 2 + cc)])
                sgb = wload_full([slab_cols(wi, C_MG + 1024 + cp * 256)])
                for cc in range(2):
                    pg = bank("proj")
                    proj_fm(sgb, cc * 128, NT, pg)
                    ACT(tmp[:, 1, :NT], ps[pg][:, :NT], AF.Sigmoid, [PK(pg)], [("tmp", 1)])
                    py = bank("proj")
                    for k in range(4):
                        MM(ps[py][:, :NT], W[:, sab, 4 + k, cc * 128:(cc + 1) * 128], yg[:, 1, k, :NT], k == 0, k == 3,
                           wkeys(sab) + YB, [PK(py)])
                    TT(tmp[:, 4, :NT], ps[py][:, :NT], tmp[:, 1, :NT], ALU.mult, [PK(py), ("tmp", 1)], [("tmp", 4)])
                    TT(tmp[:, 2 + cc, :NT], tmp[:, 2 + cc, :NT], tmp[:, 4, :NT], ALU.add, [("tmp", 2 + cc), ("tmp", 4)], [("tmp", 2 + cc)])
                sgc = wload_full([slab_cols(wi, C_MG + 2048 + cp * 256)])
                swc = wload_full([slab_cols(w_c[l], cp * 256, nk=4)])
                for cc in range(2):
                    c = cp * 2 + cc
                    pg = bank("proj")
                    proj_fm(sgc, cc * 128, NT, pg)
                    ACT(tmp[:, 1, :NT], ps[pg][:, :NT], AF.Sigmoid, [PK(pg)], [("tmp", 1)])
                    py = bank("proj")
                    for k in range(4):
                        MM(ps[py][:, :NT], W[:, swc, k, cc * 128:(cc + 1) * 128], yg[:, 2, k, :NT], k == 0, k == 3,
                           wkeys(swc) + YC, [PK(py)])
                    TT(tmp[:, 4, :NT], ps[py][:, :NT], tmp[:, 1, :NT], ALU.mult, [PK(py), ("tmp", 1)], [("tmp", 4)])
                    TT(mq[:, c, :NT], tmp[:, 2 + cc, :NT], tmp[:, 4, :NT], ALU.add, [("tmp", 2 + cc), ("tmp", 4)], [("mq", c)])
            MQ = [("mq", c) for c in range(8)]
            for cp in range(4):
                so = wload_full([slab_cols(w_o[l], cp * 256)])
                for cc in range(2):
                    c = cp * 2 + cc
                    po = bank("proj")
                    for k in range(8):
                        MM(ps[po][:, :NT], W[:, so, k, cc * 128:(cc + 1) * 128], mq[:, k, :NT], k == 0, k == 7,
                           wkeys(so) + MQ, [PK(po)])
                    STT(xT[:, c, t0:t0 + NT], ps[po][:, :NT], modT[:, l, r, 16 + c:17 + c], xT[:, c, t0:t0 + NT],
                        ALU.mult, ALU.add, [PK(po), "modT", XK(c, bidx)], [XK(c, bidx)])

        stage = tmp[:, 0:2, :]
        for bi in range(nb):
            for gt in range(18):
                src = ctx2[bi, gt * 128:(gt + 1) * 128, :] if gt < 2 else x2[bi, (gt - 2) * 128:(gt - 1) * 128, :]
                SPDMA(stage.rearrange("p a b -> p (a b)"), src, [], [("tmp", 0), ("tmp", 1)])
                bidx = 0 if gt < 2 else 1 + (gt - 2) // 4
                for hf in range(2):
                    pb = bank("proj")
                    for cq in range(4):
                        c = hf * 4 + cq
                        TR(ps[pb][:, cq * 128:(cq + 1) * 128], tmp[:, c // 4, (c % 4) * 128:(c % 4) * 128 + 128],
                           [("tmp", 0), ("tmp", 1)], [PK(pb)])
                    CPY("dve" if hf == 0 else "act", xT[:, hf * 4:hf * 4 + 4, gt * 128:(gt + 1) * 128],
                        ps[pb][:, :].rearrange("p (a b) -> p a b", a=4), [PK(pb)], [XK(c, bidx) for c in range(hf * 4, hf * 4 + 4)])
            for l in range(nl):
                last = (l == nl - 1)
                for blk in blocks:
                    phase1(l, bi, blk)
                for blk in blocks:
                    if blk[3] and last:
                        continue
                    phase2(l, bi, blk, last)
                if dbg and bi == 0 and l == 0:
                    for name, t in (("kaT", kaT), ("kcT", kcT), ("va", va), ("vc", vc), ("uT", uT), ("yg", yg), ("mq", mq),
                                    ("xT", xT), ("modT", modT), ("nlam", nlam), ("prmT", prmT)):
                        dd = dt("dbg_" + name, list(t.shape), t.dtype, "ExternalOutput")
                        dbg_t[name] = dd
                        allk = list(P.lastw.keys())
                        SPDMA(dd, t[:], allk, [])
            for j in range(4):
                bidx, t0, NT = 1 + j, L + 512 * j, 512
                for c in range(8):
                    ACT(sq[:, c % 2, :NT], xT[:, c, t0:t0 + NT], AF.Square, [XK(c, bidx)], [("sq", c % 2)])
                    MM(ps[7][:, :NT], onesM, sq[:, c % 2, :NT], c == 0, c == 7, [("sq", c % 2)] + CB, [PK(7)], signal=True)
                ACT(tmp[:, 5, :NT], ps[7][:, :NT], AF.Sqrt, [PK(7), "epsT"], [("tmp", 5)], bias=epsT[:, 0:1])
                RECIP(tmp[:, 5, :NT], tmp[:, 5, :NT], [("tmp", 5)], [("tmp", 5)])
                for tt in range(4):
                    ytile = tmp[:, 2:4, :].rearrange("p a b -> p (a b)")
                    for c in range(8):
                        STT(ytile[:, c * 128:(c + 1) * 128], xT[:, c, t0 + tt * 128:t0 + (tt + 1) * 128],
                            prmT[:, R_FN + c:R_FN + c + 1], tmp[:, 5, tt * 128:(tt + 1) * 128], ALU.mult, ALU.mult,
                            [XK(c, bidx), "prmT", ("tmp", 5)], [("tmp", 2), ("tmp", 3)])
                    for hf in range(2):
                        pb = bank("proj")
                        for cq in range(4):
                            c = hf * 4 + cq
                            TR(ps[pb][:, cq * 128:(cq + 1) * 128], ytile[:, c * 128:(c + 1) * 128],
                               [("tmp", 2), ("tmp", 3)], [PK(pb)])
                        CPY("dve" if hf == 0 else "act", tmp[:, hf, :], ps[pb][:, :], [PK(pb)], [("tmp", hf)])
                    SPDMA(out2[bi, (t0 - L) + tt * 128:(t0 - L) + (tt + 1) * 128, :], stage.rearrange("p a b -> p (a b)"),
                          [("tmp", 0), ("tmp", 1)], [])
        P.wait_all("sp", list(P.lastw.keys()) + list(P.readers.keys()))
        P.run()
    return nc, dbg_t


def rope_tables():
    inv_freq = (np.float32(10000.0) ** (-np.arange(16, dtype=np.float32) / np.float32(16))).astype(np.float32)
    t = np.arange(S)
    rows = (t // 64).astype(np.float32)
    cols = (t % 64).astype(np.float32)
    p = np.arange(128)
    d = p % 64
    f = d % 16
    axis = d // 32
    pos = np.where(axis[:, None] == 0, rows[None, :], cols[None, :]).astype(np.float32)
    ang = (pos * inv_freq[f][:, None]).astype(np.float32)
    return np.cos(ang).astype(np.float32), np.sin(ang).astype(np.float32)


def const_mats():
    c = np.zeros((6, 128, 128), np.float32)
    c[0] = np.eye(128, dtype=np.float32)
    for m in range(128):
        if m % 32 < 16:
            c[1][m + 16, m] = -1.0
        else:
            c[1][m - 16, m] = 1.0
    c[2][:] = 1.0 / 1024.0
    c[3][0:64, 0:64] = 1.0 / 64.0
    c[3][64:128, 64:128] = 1.0 / 64.0
    c[4][:] = 1.0 / 128.0
    c[5][:] = 1.0
    return c


def make_prm(inp, b0, b1, nl):
    prm = np.zeros((R_TOT, 128), np.float32)
    prm[0:8] = np.asarray(inp["c"][b0]).reshape(8, 128)
    if b1 is not None:
        prm[8:16] = np.asarray(inp["c"][b1]).reshape(8, 128)
    prm[16:24] = np.asarray(inp["c_ctx"]).reshape(8, 128)
    for l in range(nl):
        prm[R_BMOD + l * 24:R_BMOD + (l + 1) * 24] = np.asarray(inp["b_mod"][l]).reshape(24, 128)
        prm[R_CONV + l * 12:R_CONV + (l + 1) * 12] = np.asarray(inp["conv_w_b"][l]).reshape(12, 128)
        prm[R_SUBLN + l] = np.asarray(inp["subln_c"][l])
        prm[R_QN + l] = np.tile(np.asarray(inp["q_norm_a"][l]), 2)
        prm[R_KN + l] = np.tile(np.asarray(inp["k_norm_a"][l]), 2)
        for i, nm in enumerate(("lam_q1", "lam_k1", "lam_q2", "lam_k2")):
            prm[R_LAM + 4 * i + l, 0:64] = np.asarray(inp[nm][l])
    prm[R_FN:R_FN + 8] = np.asarray(inp["final_norm"]).reshape(8, 128)
    return prm


def make_in_maps(inp, ncores, nb, nl):
    rc, rs = rope_tables()
    cst = const_mats()
    f = lambda a: np.ascontiguousarray(np.asarray(a, dtype=np.float32))
    shared = dict(cst=cst, ropec=rc, ropes=rs, w_mod=f(inp["w_mod"][:nl]), w_in=f(inp["w_in"][:nl]),
                  w_a=f(inp["w_branch_a"][:nl]), w_b=f(inp["w_branch_b"][:nl]), w_c=f(inp["w_branch_c"][:nl]),
                  w_o=f(inp["w_out"][:nl]))
    maps = []
    for i in range(ncores):
        b0 = i * nb
        b1 = b0 + 1 if nb > 1 else None
        m = dict(shared)
        m["x2"] = f(inp["x"][b0:b0 + nb])
        m["ctx2"] = f(inp["ctx"][b0:b0 + nb])
        m["prm"] = make_prm(inp, b0, b1, nl)
        maps.append(m)
    return maps


def kernel(**inputs):
    nc, _ = build_program(DEPTH, 2)
    maps = make_in_maps(inputs, NCORES, 2, DEPTH)
    res = run_bass_kernel_spmd(nc, maps, core_ids=list(range(NCORES)))
    out = np.concatenate([np.asarray(r["out2"]) for r in res.results], axis=0)
    return out.astype(np.float32)
```

```python
import math
from contextlib import ExitStack

import numpy as np
import concourse.bass as bass
import concourse.mybir as mybir
from concourse.bass_utils import run_bass_kernel_spmd

F32 = mybir.dt.float32
BF16 = mybir.dt.bfloat16
AF = mybir.ActivationFunctionType
ALU = mybir.AluOpType

D = 1024
S = 2048
L = 256
T = S + L
DEPTH = 4
NCORES = 8
PT = 8448
EPS = 1e-6
SCALE = 64 ** -0.5
NSLOT = 3
C_AQ, C_AK, C_AV, C_AG = 0, 512, 640, 768
C_BH, C_BB, C_BC, C_BG = 1280, 1792, 2304, 2816
C_CQ, C_CK, C_CV, C_CG = 3328, 3840, 4352, 4864
C_MG = 5376
R_C, R_BMOD, R_CONV, R_SUBLN, R_FN, R_QN, R_KN, R_LAM, R_TOT = 0, 24, 120, 168, 172, 180, 184, 188, 204


class Sem:
    def __init__(self, h, name):
        self.h = h
        self.v = 0
        self.name = name


class Prog:
    ENGS = ("pe", "act", "dve", "pool", "sp")

    def __init__(self, nc, stack):
        self.nc = nc
        self.stack = stack
        self.streams = {e: [] for e in self.ENGS}
        self.esem = {e: self.new_sem("e_" + e) for e in self.ENGS if e != "sp"}
        self.waited = {}
        self.lastw = {}
        self.readers = {}

    def new_sem(self, name):
        return Sem(self.stack.enter_context(self.nc.semaphore(name)), name)

    def _deps(self, eng, reads, writes):
        need = {}

        def add(tok, war=False):
            if tok is None:
                return
            s, v, teng = tok
            if teng == eng and eng == "pe":
                return
            if need.get(s, 0) < v:
                need[s] = v

        for k in reads:
            add(self.lastw.get(k))
        for k in writes:
            add(self.lastw.get(k))
            for t in self.readers.get(k, ()):
                add(t, war=True)
        out = []
        for s, v in need.items():
            if self.waited.get((eng, s), 0) < v:
                self.waited[(eng, s)] = v
                out.append((s, v))
        return out

    def _record(self, tok, reads, writes):
        for k in writes:
            self.lastw[k] = tok
            self.readers[k] = []
        for k in reads:
            self.readers.setdefault(k, []).append(tok)

    def op(self, eng, emit, reads=(), writes=(), signal=True):
        waits = self._deps(eng, reads, writes)
        s = self.esem[eng]
        if signal:
            s.v += 1
            tok = (s, s.v, eng)
            self.streams[eng].append((waits, emit, (s, 1)))
        else:
            tok = (s, s.v + 1, eng)
            self.streams[eng].append((waits, emit, None))
        self._record(tok, reads, writes)

    def dma(self, eng, emit, dsem, reads=(), writes=()):
        waits = self._deps(eng, reads, writes)
        if dsem.v > 0 and self.waited.get((eng, dsem), 0) < dsem.v:
            self.waited[(eng, dsem)] = dsem.v
            waits.append((dsem, dsem.v))
        dsem.v += 16
        tok = (dsem, dsem.v, "dma")
        self.streams[eng].append((waits, emit, (dsem, 16)))
        self._record(tok, reads, writes)

    def wait_all(self, eng, keys):
        waits = self._deps(eng, (), keys)
        self.streams[eng].append((waits, None, None))

    def replay(self, eng, e):
        for waits, emit, inc in self.streams[eng]:
            for s, v in waits:
                e.wait_ge(s.h, v)
            if emit is None:
                continue
            ins = emit(e)
            if inc is not None:
                ins.then_inc(inc[0].h, inc[1])

    def run(self):
        with self.nc.Block() as block:
            @block.tensor
            def _(e):
                self.replay("pe", e)

            @block.scalar
            def _(e):
                self.replay("act", e)

            @block.vector
            def _(e):
                self.replay("dve", e)

            @block.gpsimd
            def _(e):
                self.replay("pool", e)

            @block.sync
            def _(e):
                self.replay("sp", e)


def build_program(nl=DEPTH, nb=2, dbg=False):
    nc = bass.Bass("TRN2", target_bir_lowering=False)
    dt = lambda name, shape, dtype, kind: nc.dram_tensor(name, shape, dtype, kind=kind).ap()
    x2 = dt("x2", [nb, S, D], F32, "ExternalInput")
    ctx2 = dt("ctx2", [nb, L, D], F32, "ExternalInput")
    prm = dt("prm", [R_TOT, 128], F32, "ExternalInput")
    cst = dt("cst", [6, 128, 128], F32, "ExternalInput")
    ropec = dt("ropec", [128, S], F32, "ExternalInput")
    ropes = dt("ropes", [128, S], F32, "ExternalInput")
    w_mod = dt("w_mod", [nl, D, 3 * D], F32, "ExternalInput")
    w_in = dt("w_in", [nl, D, PT], F32, "ExternalInput")
    w_a = dt("w_a", [nl, 512, D], F32, "ExternalInput")
    w_b = dt("w_b", [nl, 512, D], F32, "ExternalInput")
    w_c = dt("w_c", [nl, 512, D], F32, "ExternalInput")
    w_o = dt("w_o", [nl, D, D], F32, "ExternalInput")
    out2 = dt("out2", [nb, S, D], F32, "ExternalOutput")
    dbg_t = {}

    with ExitStack() as st:
        P = Prog(nc, st)
        sb = lambda name, shape, dtype: st.enter_context(nc.sbuf_tensor(name, shape, dtype))
        xT = sb("xT", [128, 8, T], F32)
        kaT = sb("kaT", [128, T], BF16)
        va = sb("va", [128, 18, 2, 128], BF16)
        kcT = sb("kcT", [128, 4, T], BF16)
        vc = sb("vc", [128, 18, 512], BF16)
        uT = sb("uT", [128, 4, T + 4], BF16)
        hx = sb("hx", [128, 8, 512], BF16)
        mq = sb("mq", [128, 8, 512], BF16)
        yg = sb("yg", [128, 3, 4, 512], BF16)
        pT = sb("pT", [128, 6, 512], BF16)
        tmp = sb("tmp", [128, 6, 512], F32)
        sq1 = sb("sq1", [128, 512], BF16)
        cosb = sb("cosb", [128, 512], F32)
        sinb = sb("sinb", [128, 512], F32)
        W = sb("W", [128, NSLOT, 8, 256], BF16)
        cF = sb("cF", [128, 2, 128], F32)
        cB = sb("cB", [128, 4, 128], BF16)
        prmT = sb("prmT", [128, R_TOT], F32)
        scT = sb("scT", [128, 24], BF16)
        modT = sb("modT", [128, nl, 3, 24], F32)
        ops1 = sb("ops1", [128, nl, 3, 8], F32)
        lamw = sb("lamw", [128, 16], F32)
        nlam = sb("nlam", [128, 4], F32)
        subS = sb("subS", [128, 4], F32)
        epsT = sb("epsT", [128, 1], F32)
        ps = [st.enter_context(nc.psum_tensor(f"ps{i}", [128, 512], F32)) for i in range(8)]

        def SQ(i):
            return pT[:, 5, :] if i == 0 else sq1[:, :]

        def SQK(i):
            return ("pT", 5) if i == 0 else ("sq", 1)

        identF = cF[:, 0, :]
        permF = cF[:, 1, :]
        onesM, ones64, ones128, ones1 = cB[:, 0, :], cB[:, 1, :], cB[:, 2, :], cB[:, 3, :]

        phase_log = []
        cur_phase = ["setup"]

        def set_phase(name):
            cur_phase[0] = name

        def MM(out, lhsT, rhs, start, stop, reads, writes, signal=None):
            phase_log.append(cur_phase[0])
            P.op("pe", lambda e: e.matmul(out, lhsT, rhs, start=start, stop=stop), reads=reads, writes=writes,
                 signal=stop if signal is None else signal)

        def TR(out, in_, reads, writes):
            n_ = in_.shape[0]
            phase_log.append(cur_phase[0])
            P.op("pe", lambda e: e.transpose(out, in_, identF[0:n_, 0:n_]), reads=list(reads) + ["cF"], writes=writes)

        def ACT(out, in_, func, reads, writes, bias=None, scale=1.0):
            if bias is None:
                P.op("act", lambda e: e.activation(out=out, in_=in_, func=func, scale=scale), reads=reads, writes=writes)
            else:
                P.op("act", lambda e: e.activation(out=out, in_=in_, func=func, bias=bias, scale=scale), reads=reads, writes=writes)

        def TT(out, a, b, op, reads, writes, eng="dve"):
            P.op(eng, lambda e: e.tensor_tensor(out, a, b, op), reads=reads, writes=writes)

        def TS(out, in0, s1, s2, op0, op1, reads, writes):
            P.op("dve", lambda e: e.tensor_scalar(out, in0, s1, s2, op0, op1), reads=reads, writes=writes)

        def STT(out, in0, scalar, in1, op0, op1, reads, writes):
            P.op("dve", lambda e: e.scalar_tensor_tensor(out=out, in0=in0, scalar=scalar, in1=in1, op0=op0, op1=op1),
                 reads=reads, writes=writes)

        def RECIP(out, in_, reads, writes):
            P.op("dve", lambda e: e.reciprocal(out, in_), reads=reads, writes=writes)

        def CPY(eng, out, in_, reads, writes):
            if eng == "act":
                ACT(out, in_, AF.Copy, reads, writes)
            else:
                P.op("dve", lambda e: e.tensor_copy(out, in_), reads=reads, writes=writes)

        def MEMSET(out, val, writes):
            P.op("dve", lambda e: e.memset(out, val), writes=writes)

        gsem = [P.new_sem(f"g{i}") for i in range(8)]
        gctr = [0]

        def SPDMA(out, in_, reads, writes):
            s = gsem[gctr[0] % len(gsem)]
            gctr[0] += 1
            P.dma("sp", lambda e: e.dma_start(out=out, in_=in_), s, reads=reads, writes=writes)

        rr = {"proj": 0, "score": 0, "acc": 0, "scoreA": 0, "accA": 0}
        pools = {"proj": [0, 1, 2, 3, 4, 5, 6, 7], "score": [0, 1, 2, 3], "acc": [4, 5, 6, 7],
                 "scoreA": [0, 1, 2, 3, 4, 5], "accA": [6, 7]}

        def bank(pool):
            b = pools[pool][rr[pool] % len(pools[pool])]
            rr[pool] += 1
            return b

        PK = lambda b: ("ps", b)
        ptc = [0]

        def next_pt():
            i = ptc[0] % 6
            ptc[0] += 1
            return i

        tmc = [0]

        wsem = [[P.new_sem(f"w{s}_{i}") for i in range(9)] for s in range(NSLOT)]
        wctr = [0]

        def wkeys(slot):
            return [("w", slot, i) for i in range(9)]

        def wload(pieces):
            slot = wctr[0] % NSLOT
            wctr[0] += 1
            assert len(pieces) <= 9
            for i, (dstf, src) in enumerate(pieces):
                dst = dstf(slot)
                P.dma("pool", (lambda e, dst=dst, src=src: e.dma_start(out=dst, in_=src)), wsem[slot][i],
                      reads=(), writes=[("w", slot, i)])
            return slot

        def wload_full(pieces):
            slot = wctr[0] % NSLOT
            wctr[0] += 1
            n = len(pieces)
            assert n <= 9
            for i, (dstf, src) in enumerate(pieces):
                dst = dstf(slot)
                wr = [("w", slot, i)]
                if i == 0:
                    wr += [("w", slot, j) for j in range(n, 9)]
                P.dma("pool", (lambda e, dst=dst, src=src: e.dma_start(out=dst, in_=src)), wsem[slot][i],
                      reads=(), writes=wr)
            return slot

        def slab_cols(wmat, col0, ncols=256, nk=8, koff=0):
            src = wmat[:, col0:col0 + ncols].rearrange("(k p) c -> p k c", p=128)
            return (lambda slot: W[:, slot, koff:koff + nk, 0:ncols], src)

        def slab_pairmajor_cols(wmat, base, half):
            pcs = []
            for jj in range(2):
                j = 2 * half + jj
                for hp in range(2):
                    c0 = base + hp * 256 + j * 64
                    src = wmat[:, c0:c0 + 64].rearrange("(k p) c -> p k c", p=128)
                    d0 = jj * 128 + hp * 64
                    pcs.append((lambda slot, d0=d0: W[:, slot, :, d0:d0 + 64], src))
            return pcs

        SPDMA(cF[:], cst[0:2].rearrange("a p c -> p a c"), [], ["cF"])
        for i in range(4):
            P.dma("pool", (lambda e, i=i: e.dma_start(out=cB[:, i, :], in_=cst[2 + i])), wsem[0][i], writes=[("cB", i)])
        CB = [("cB", i) for i in range(4)]
        SPDMA(tmp[:, 5, 0:128], prm[0:128, :], [], [("tmp", 5)])
        MEMSET(epsT[:], EPS, ["epsT"])
        MEMSET(uT[:], 0.0, ["uT"])
        MEMSET(va[:, :, :, 64:128], 1.0, ["va1"])
        TR(ps[0][:, 0:128], tmp[:, 5, 0:128], [("tmp", 5)], [PK(0)])
        SPDMA(tmp[0:R_TOT - 128, 5, 0:128], prm[128:R_TOT, :], [], [("tmp", 5)])
        TR(ps[1][:, 0:R_TOT - 128], tmp[0:R_TOT - 128, 5, 0:128], [("tmp", 5)], [PK(1)])
        CPY("dve", prmT[:, 0:128], ps[0][:, 0:128], [PK(0)], ["prmT"])
        CPY("dve", prmT[:, 128:R_TOT], ps[1][:, 0:R_TOT - 128], [PK(1)], ["prmT"])
        ACT(scT[:], prmT[:, 0:24], AF.Silu, ["prmT"], ["scT"])
        for l in range(nl):
            pb = bank("proj")
            for sl in range(12):
                slot = wload_full([slab_cols(w_mod[l], sl * 256)])
                for cc in range(2):
                    jc = sl * 2 + cc
                    for k in range(8):
                        MM(ps[pb][:, jc * 3:jc * 3 + 3], W[:, slot, k, cc * 128:(cc + 1) * 128], scT[:, k:24:8],
                           k == 0, k == 7, wkeys(slot) + ["scT"], [PK(pb)])
            for r in range(3):
                TT(modT[:, l, r, :], ps[pb][:, r:72:3], prmT[:, R_BMOD + l * 24:R_BMOD + (l + 1) * 24], ALU.add,
                   [PK(pb), "prmT"], ["modT"])
                TS(ops1[:, l, r, :], modT[:, l, r, 8:16], 1.0, 0.0, ALU.add, ALU.add, ["modT"], ["ops1"])
        TT(lamw[:, 0:4], prmT[:, R_LAM:R_LAM + 4], prmT[:, R_LAM + 4:R_LAM + 8], ALU.mult, ["prmT"], ["lamw"])
        TT(lamw[:, 4:8], prmT[:, R_LAM + 8:R_LAM + 12], prmT[:, R_LAM + 12:R_LAM + 16], ALU.mult, ["prmT"], ["lamw"])
        MEMSET(tmp[:, 0, 0:128], 1.0, [("tmp", 0)])
        MM(ps[7][:, 0:8], tmp[:, 0, 0:128], lamw[:, 0:8], True, True, [("tmp", 0), "lamw"], [PK(7)])
        ACT(lamw[:, 8:16], ps[7][:, 0:8], AF.Exp, [PK(7)], ["lamw2"])
        TT(nlam[:], lamw[:, 12:16], lamw[:, 8:12], ALU.subtract, ["lamw2"], ["nlam"])
        for l in range(4):
            li = 0.8 - 0.6 * math.exp(-0.3 * l)
            TS(nlam[:, l:l + 1], nlam[:, l:l + 1], -li, 0.0, ALU.add, ALU.add, ["nlam"], ["nlam"])
            TS(subS[:, l:l + 1], prmT[:, R_SUBLN + l:R_SUBLN + l + 1], 1.0 - li, 0.0, ALU.mult, ALU.add, ["prmT"], ["subS"])

        XK = lambda c, bidx: ("x", c, bidx)
        blocks = [(0, 0, L, True)] + [(1 + j, L + 512 * j, 512, False) for j in range(4)]

        def ucol(t0, isctx):
            return 1 + t0 if isctx else 3 + t0

        def load_rope(t0, NT):
            SPDMA(cosb[:, :NT], ropec[:, t0 - L:t0 - L + NT], [], ["cosb"])
            SPDMA(sinb[:, :NT], ropes[:, t0 - L:t0 - L + NT], [], ["sinb"])

        def hx_block(l, r, bidx, t0, NT):
            set_phase(cur_phase[0].split('.')[0] + '.hx')
            for c in range(8):
                ACT(SQ(c % 2)[:, :NT], xT[:, c, t0:t0 + NT], AF.Square, [XK(c, bidx)], [SQK(c % 2)])
                MM(ps[7][:, :NT], onesM, SQ(c % 2)[:, :NT], c == 0, c == 7, [SQK(c % 2)] + CB, [PK(7)], signal=True)
            ACT(tmp[:, 5, :NT], ps[7][:, :NT], AF.Sqrt, [PK(7), "epsT"], [("tmp", 5)], bias=epsT[:, 0:1])
            RECIP(tmp[:, 5, :NT], tmp[:, 5, :NT], [("tmp", 5)], [("tmp", 5)])
            for c in range(8):
                ti = c % 2
                STT(tmp[:, ti, :NT], xT[:, c, t0:t0 + NT], ops1[:, l, r, c:c + 1], tmp[:, 5, :NT], ALU.mult, ALU.mult,
                    [XK(c, bidx), "ops1", ("tmp", 5)], [("tmp", ti)])
                ACT(hx[:, c, :NT], tmp[:, ti, :NT], AF.Identity, [("tmp", ti), "modT"], [("hx", c)],
                    bias=modT[:, l, r, c:c + 1])

        HXK = [("hx", c) for c in range(8)]

        def proj_fm(slot, coff, NT, pb):
            for k in range(8):
                MM(ps[pb][:, :NT], W[:, slot, k, coff:coff + 128], hx[:, k, :NT], k == 0, k == 7,
                   wkeys(slot) + HXK, [PK(pb)])

        def rope_tail(src_f32, src_key, dst, dst_key, NT, rstd=None):
            pr = bank("proj")
            MM(ps[pr][:, :NT], permF, src_f32, True, True, [src_key, "cF"], [PK(pr)])
            TT(tmp[:, 2, :NT], src_f32, cosb[:, :NT], ALU.mult, [src_key, "cosb"], [("tmp", 2)])
            TT(tmp[:, 3, :NT], ps[pr][:, :NT], sinb[:, :NT], ALU.mult, [PK(pr), "sinb"], [("tmp", 3)])
            if rstd is None:
                TT(dst, tmp[:, 2, :NT], tmp[:, 3, :NT], ALU.add, [("tmp", 2), ("tmp", 3)], [dst_key])
            else:
                TT(tmp[:, 2, :NT], tmp[:, 2, :NT], tmp[:, 3, :NT], ALU.add, [("tmp", 2), ("tmp", 3)], [("tmp", 2)])
                TT(dst, tmp[:, 2, :NT], rstd, ALU.mult, [("tmp", 2), ("tmp", 4)], [dst_key])

        def post_a(pb, gcol, dst, dst_key, NT, isctx):
            ACT(SQ(0)[:, :NT], ps[pb][:, :NT], AF.Square, [PK(pb)], [SQK(0)])
            ACT(tmp[:, 0, :NT], ps[pb][:, :NT], AF.Identity, [PK(pb), "prmT"], [("tmp", 0)], scale=prmT[:, gcol:gcol + 1])
            MM(ps[7][:, :NT], ones64, SQ(0)[:, :NT], True, True, [SQK(0)] + CB, [PK(7)])
            ACT(tmp[:, 4, :NT], ps[7][:, :NT], AF.Sqrt, [PK(7), "epsT"], [("tmp", 4)], bias=epsT[:, 0:1])
            RECIP(tmp[:, 4, :NT], tmp[:, 4, :NT], [("tmp", 4)], [("tmp", 4)])
            if isctx:
                TT(dst, tmp[:, 0, :NT], tmp[:, 4, :NT], ALU.mult, [("tmp", 0), ("tmp", 4)], [dst_key])
            else:
                rope_tail(tmp[:, 0, :NT], ("tmp", 0), dst, dst_key, NT, rstd=tmp[:, 4, :NT])

        def post_c(pb, dst, dst_key, NT, isctx):
            if isctx:
                CPY("act", dst, ps[pb][:, :NT], [PK(pb)], [dst_key])
            else:
                CPY("act", tmp[:, 0, :NT], ps[pb][:, :NT], [PK(pb)], [("tmp", 0)])
                rope_tail(tmp[:, 0, :NT], ("tmp", 0), dst, dst_key, NT)

        def phase1(l, bi, blk):
            bidx, t0, NT, isctx = blk
            r = 2 if isctx else bi
            wi = w_in[l]
            set_phase('p1')
            if not isctx:
                load_rope(t0, NT)
            hx_block(l, r, bidx, t0, NT)
            ntile = NT // 128
            set_phase('p1.kava')
            slot = wload_full([slab_cols(wi, C_AK)])
            pb = bank("proj")
            proj_fm(slot, 0, NT, pb)
            post_a(pb, R_KN + l, kaT[:, t0:t0 + NT], ("kaT", bidx), NT, isctx)
            for tt in range(ntile):
                pb = bank("proj")
                for k in range(8):
                    MM(ps[pb][:, 0:128], hx[:, k, tt * 128:(tt + 1) * 128], W[:, slot, k, 128:256], k == 0, k == 7,
                       wkeys(slot) + HXK, [PK(pb)])
                gt = t0 // 128 + tt
                for g in range(2):
                    CPY("act" if g == 0 else "dve", va[:, gt, g, 0:64], ps[pb][:, g * 64:(g + 1) * 64], [PK(pb)],
                        [("va", gt, g)])
            set_phase('p1.kc')
            for sl in range(2):
                slot = wload_full([slab_cols(wi, C_CK + sl * 256)])
                for cc in range(2):
                    h = sl * 2 + cc
                    pb = bank("proj")
                    proj_fm(slot, cc * 128, NT, pb)
                    post_c(pb, kcT[:, h, t0:t0 + NT], ("kcT", h, bidx), NT, isctx)
            set_phase('p1.vc')
            s0 = wload_full([slab_cols(wi, C_CV)])
            s1 = wload_full([slab_cols(wi, C_CV + 256)])
            for tt in range(ntile):
                pb = bank("proj")
                for hf, slot in enumerate((s0, s1)):
                    for k in range(8):
                        MM(ps[pb][:, hf * 256:(hf + 1) * 256], hx[:, k, tt * 128:(tt + 1) * 128], W[:, slot, k, :],
                           k == 0, k == 7, wkeys(slot) + HXK, [PK(pb)], signal=(k == 7))
                gt = t0 // 128 + tt
                CPY("act" if tt % 2 == 0 else "dve", vc[:, gt, :], ps[pb][:, :], [PK(pb)], [("vc", gt)])
            set_phase('p1.u')
            uc = ucol(t0, isctx)
            for sl in range(2):
                sh = wload_full([slab_cols(wi, C_BH + sl * 256)])
                sc_ = wload_full([slab_cols(wi, C_BC + sl * 256)])
                for cc in range(2):
                    c = sl * 2 + cc
                    p1 = bank("proj")
                    proj_fm(sh, cc * 128, NT, p1)
                    p2 = bank("proj")
                    proj_fm(sc_, cc * 128, NT, p2)
                    CPY("act", tmp[:, 1, :NT], ps[p1][:, :NT], [PK(p1)], [("tmp", 1)])
                    TT(uT[:, c, uc:uc + NT], ps[p2][:, :NT], tmp[:, 1, :NT], ALU.mult, [PK(p2), ("tmp", 1)], [("uT", c, bidx), "uT"])

        def attn_unit(k_ap_fn, q_ap, q_keys, k_keys_fn, pv_list, NT, tiles, G=2, spool="score", hook=None):
            groups = [tiles[i:i + G] for i in range(0, len(tiles), G)]
            n = len(tiles)
            pend = None
            idx = 0
            for gi in range(len(groups) + 1):
                cur = None
                if gi < len(groups):
                    g = groups[gi]
                    banks = [bank(spool) for _ in g]
                    pis = [next_pt() for _ in g]
                    allk = []
                    for stt in g:
                        allk += list(k_keys_fn(stt))
                    for j, stt in enumerate(g):
                        wr = [PK(b) for b in banks] if j == 0 else [PK(banks[j])]
                        rd = list(q_keys) + (allk if j == 0 else list(k_keys_fn(stt)))
                        MM(ps[banks[j]][:, :NT], k_ap_fn(stt), q_ap, True, True, rd, wr)
                    for j, stt in enumerate(g):
                        ACT(pT[:, pis[j], :NT], ps[banks[j]][:, :NT], AF.Exp, [PK(banks[j])], [("pT", pis[j])], scale=SCALE)
                    cur = (g, pis, idx)
                    idx += len(g)
                if pend is not None:
                    pg, ppis, pidx = pend
                    first = True
                    for j, pst in enumerate(pg):
                        for (ab, lf, kf) in pv_list:
                            rd = [("pT", p) for p in ppis] if first else [("pT", ppis[j])]
                            first = False
                            MM(ps[ab][:, :NT], lf(pst), pT[:, ppis[j], :NT], pidx + j == 0, pidx + j == n - 1,
                               rd + list(kf(pst)), [PK(ab)])
                pend = cur
                if hook is not None and gi == min(2, len(groups) - 1):
                    hook()

        def phase2(l, bi, blk, last):
            bidx, t0, NT, isctx = blk
            r = 2 if isctx else bi
            wi = w_in[l]
            tiles = [0, 1] if isctx else list(range(18))
            tblk = lambda tl: 0 if tl < 2 else 1 + (tl - 2) // 4
            set_phase('p2')
            if not isctx:
                load_rope(t0, NT)
            hx_block(l, r, bidx, t0, NT)
            set_phase('p2.A')
            for half in range(2):
                slot = wload_full(slab_pairmajor_cols(wi, C_AQ, half))
                for jj in range(2):
                    j = 2 * half + jj
                    pb = bank("proj")
                    proj_fm(slot, jj * 128, NT, pb)
                    post_a(pb, R_QN + l, mq[:, j, :NT], ("mq", j), NT, isctx)
            for half in range(2):
                slot = wload_full(slab_pairmajor_cols(wi, C_AG, half))
                for jj in range(2):
                    j = 2 * half + jj
                    pb = bank("proj")
                    proj_fm(slot, jj * 128, NT, pb)
                    gb_ = 1 - (j % 2)
                    ACT(tmp[:, gb_, :NT], ps[pb][:, :NT], AF.Silu, [PK(pb)], [("tmp", gb_)])
                    for hh in range(2):
                        ab = bank("accA")
                        lo, hi = 64 * hh, 64 * hh + 64
                        attn_unit(lambda tl, lo=lo, hi=hi: kaT[lo:hi, tl * 128:(tl + 1) * 128],
                                  mq[lo:hi, j, :NT], [("mq", j)],
                                  lambda tl: [("kaT", tblk(tl))],
                                  [(ab, (lambda tl, hh=hh: va[:, tl, hh, :]), (lambda tl, hh=hh: [("va", tl, hh), "va1"]))],
                                  NT, tiles, G=3, spool="scoreA")
                        RECIP(tmp[0:64, 2, :NT], ps[ab][64:128, :NT], [PK(ab)], [("tmp", 2)])
                        TT(tmp[lo:hi, 3, :NT], ps[ab][0:64, :NT], tmp[0:64, 2, :NT], ALU.mult, [PK(ab), ("tmp", 2)], [("tmp", 3)])
                        TT(yg[lo:hi, 0, j, :NT], tmp[lo:hi, 3, :NT], tmp[lo:hi, gb_, :NT], ALU.mult,
                           [("tmp", 3), ("tmp", gb_)], [("yg", 0, j, hh)])
            set_phase('p2.C')
            ctail = [None]
            for sl in range(2):
                slot = wload_full([slab_cols(wi, C_CQ + sl * 256)])
                for cc in range(2):
                    h = sl * 2 + cc
                    pb = bank("proj")
                    proj_fm(slot, cc * 128, NT, pb)
                    post_c(pb, mq[:, h, :NT], ("mq", h), NT, isctx)
            for sl in range(2):
                slot = wload_full([slab_cols(wi, C_CG + sl * 256)])
                for cc in range(2):
                    h = sl * 2 + cc
                    pb = bank("proj")
                    proj_fm(slot, cc * 128, NT, pb)
                    gbc = 1 - (h % 2)
                    ACT(tmp[:, gbc, :NT], ps[pb][:, :NT], AF.Silu, [PK(pb)], [("tmp", gbc)])
                    accs = [bank("acc") for _ in range(4)]
                    for m in range(2):
                        lo, hi = 64 * m, 64 * m + 64
                        hk = None
                        if m == 0 and ctail[0] is not None:
                            hk, ctail[0] = ctail[0], None
                        attn_unit(lambda tl, lo=lo, hi=hi, h=h: kcT[lo:hi, h, tl * 128:(tl + 1) * 128],
                                  mq[lo:hi, h, :NT], [("mq", h)],
                                  lambda tl, h=h: [("kcT", h, tblk(tl))],
                                  [(accs[2 * m], (lambda tl, h=h: vc[:, tl, h * 128:(h + 1) * 128]), (lambda tl: [("vc", tl)])),
                                   (accs[2 * m + 1], (lambda tl: ones1), (lambda tl: CB))],
                                  NT, tiles, hook=hk)
                        if m == 0:
                            RECIP(tmp[:, 2, :NT], ps[accs[1]][:, :NT], [PK(accs[1])], [("tmp", 2)])
                            TT(tmp[:, 2, :NT], ps[accs[0]][:, :NT], tmp[:, 2, :NT], ALU.mult, [PK(accs[0]), ("tmp", 2)], [("tmp", 2)])
                    RECIP(tmp[:, 3, :NT], ps[accs[3]][:, :NT], [PK(accs[3])], [("tmp", 3)])
                    TT(tmp[:, 3, :NT], ps[accs[2]][:, :NT], tmp[:, 3, :NT], ALU.mult, [PK(accs[2]), ("tmp", 3)], [("tmp", 3)])
                    STT(tmp[:, 2, :NT], tmp[:, 3, :NT], nlam[:, l:l + 1], tmp[:, 2, :NT], ALU.mult, ALU.add,
                        [("tmp", 2), ("tmp", 3), "nlam"], [("tmp", 2)])

                    def tail(h=h, gbc=gbc):
                        ACT(SQ(1)[:, :NT], tmp[:, 2, :NT], AF.Square, [("tmp", 2)], [("sq", 1)])
                        sbn = bank("score")
                        MM(ps[sbn][:, :NT], ones128, SQ(1)[:, :NT], True, True, [("sq", 1)] + CB, [PK(sbn)])
                        ACT(tmp[:, 4, :NT], ps[sbn][:, :NT], AF.Sqrt, [PK(sbn), "epsT"], [("tmp", 4)], bias=epsT[:, 0:1])
                        RECIP(tmp[:, 4, :NT], tmp[:, 4, :NT], [("tmp", 4)], [("tmp", 4)])
                        STT(tmp[:, 2, :NT], tmp[:, 2, :NT], subS[:, l:l + 1], tmp[:, 4, :NT], ALU.mult, ALU.mult,
                            [("tmp", 2), ("tmp", 4), "subS"], [("tmp", 2)])
                        TT(yg[:, 2, h, :NT], tmp[:, 2, :NT], tmp[:, gbc, :NT], ALU.mult, [("tmp", 2), ("tmp", gbc)], [("yg", 2, h)])
                    ctail[0] = tail
            if ctail[0] is not None:
                ctail[0]()
                ctail[0] = None
            set_phase('p2.B')
            uc = ucol(t0, isctx)
            for sl in range(2):
                sbb = wload_full([slab_cols(wi, C_BB + sl * 256)])
                sbg = wload_full([slab_cols(wi, C_BG + sl * 256)])
                for cc in range(2):
                    c = sl * 2 + cc
                    p1 = bank("proj")
                    proj_fm(sbb, cc * 128, NT, p1)
                    p2 = bank("proj")
                    proj_fm(sbg, cc * 128, NT, p2)
                    ACT(tmp[:, 1, :NT], ps[p2][:, :NT], AF.Silu, [PK(p2)], [("tmp", 1)])
                    cw = lambda tap: prmT[:, R_CONV + l * 12 + tap * 4 + c:R_CONV + l * 12 + tap * 4 + c + 1]
                    TS(tmp[:, 2, :NT], uT[:, c, uc - 1:uc - 1 + NT], cw(0), 0.0, ALU.mult, ALU.add, ["uT", "prmT"], [("tmp", 2)])
                    STT(tmp[:, 3, :NT], uT[:, c, uc:uc + NT], cw(1), tmp[:, 2, :NT], ALU.mult, ALU.add,
                        ["uT", "prmT", ("tmp", 2)], [("tmp", 3)])
                    STT(tmp[:, 2, :NT], uT[:, c, uc + 1:uc + 1 + NT], cw(2), tmp[:, 3, :NT], ALU.mult, ALU.add,
                        ["uT", "prmT", ("tmp", 3)], [("tmp", 2)])
                    TT(tmp[:, 3, :NT], ps[p1][:, :NT], tmp[:, 2, :NT], ALU.mult, [PK(p1), ("tmp", 2)], [("tmp", 3)])
                    TT(yg[:, 1, c, :NT], tmp[:, 3, :NT], tmp[:, 1, :NT], ALU.mult, [("tmp", 3), ("tmp", 1)], [("yg", 1, c)])
            set_phase('p2.merge')
            YA = [("yg", 0, j, hh) for j in range(4) for hh in range(2)]
            YB = [("yg", 1, c) for c in range(4)]
            YC = [("yg", 2, c) for c in range(4)]
            for cp in range(4):
                pcs = []
                for k in range(4):
                    for hp in range(2):
                        r0 = hp * 256 + k * 64
                        pcs.append((lambda slot, k=k, hp=hp: W[hp * 64:hp * 64 + 64, slot, k, :],
                                    w_a[l][r0:r0 + 64, cp * 256:(cp + 1) * 256]))
                pcs.append(slab_cols(w_b[l], cp * 256, nk=4, koff=4))
                sga = wload_full([slab_cols(wi, C_MG + cp * 256)])
                sab = wload_full(pcs)
                for cc in range(2):
                    c = cp * 2 + cc
                    pg = bank("proj")
                    proj_fm(sga, cc * 128, NT, pg)
                    ACT(tmp[:, 1, :NT], ps[pg][:, :NT], AF.Sigmoid, [PK(pg)], [("tmp", 1)])
                    py = bank("proj")
                    for k in range(4):
                        MM(ps[py][:, :NT], W[:, sab, k, cc * 128:(cc + 1) * 128], yg[:, 0, k, :NT], k == 0, k == 3,
                           wkeys(sab) + YA, [PK(py)])
                    TT(tmp[:, 2 + cc, :NT], ps[py][:, :NT], tmp[:, 1, :NT], ALU.mult, [PK(py), ("tmp", 1)], [("tmp", 2 + cc)])
                sgb = wload_full([slab_cols(wi, C_MG + 1024 + cp * 256)])
                for cc in range(2):
                    pg = bank("proj")
                    proj_fm(sgb, cc * 128, NT, pg)
                    ACT(tmp[:, 1, :NT], ps[pg][:, :NT], AF.Sigmoid, [PK(pg)], [("tmp", 1)])
                    py = bank("proj")
                    for k in range(4):
                        MM(ps[py][:, :NT], W[:, sab, 4 + k, cc * 128:(cc + 1) * 128], yg[:, 1, k, :NT], k == 0, k == 3,
                           wkeys(sab) + YB, [PK(py)])
                    TT(tmp[:, 4, :NT], ps[py][:, :NT], tmp[:, 1, :NT], ALU.mult, [PK(py), ("tmp", 1)], [("tmp", 4)])
                    TT(tmp[:, 2 + cc, :NT], tmp[:, 2 + cc, :NT], tmp[:, 4, :NT], ALU.add, [("tmp", 2 + cc), ("tmp", 4)], [("tmp", 2 + cc)])
                sgc = wload_full([slab_cols(wi, C_MG + 2048 + cp * 256)])
                swc = wload_full([slab_cols(w_c[l], cp * 256, nk=4)])
                for cc in range(2):
                    c = cp * 2 + cc
                    pg = bank("proj")
                    proj_fm(sgc, cc * 128, NT, pg)
                    ACT(tmp[:, 1, :NT], ps[pg][:, :NT], AF.Sigmoid, [PK(pg)], [("tmp", 1)])
                    py = bank("proj")
                    for k in range(4):
                        MM(ps[py][:, :NT], W[:, swc, k, cc * 128:(cc + 1) * 128], yg[:, 2, k, :NT], k == 0, k == 3,
                           wkeys(swc) + YC, [PK(py)])
                    TT(tmp[:, 4, :NT], ps[py][:, :NT], tmp[:, 1, :NT], ALU.mult, [PK(py), ("tmp", 1)], [("tmp", 4)])
                    TT(mq[:, c, :NT], tmp[:, 2 + cc, :NT], tmp[:, 4, :NT], ALU.add, [("tmp", 2 + cc), ("tmp", 4)], [("mq", c)])
            set_phase('p2.out')
            MQ = [("mq", c) for c in range(8)]
            for cp in range(4):
                so = wload_full([slab_cols(w_o[l], cp * 256)])
                for cc in range(2):
                    c = cp * 2 + cc
                    po = bank("proj")
                    for k in range(8):
                        MM(ps[po][:, :NT], W[:, so, k, cc * 128:(cc + 1) * 128], mq[:, k, :NT], k == 0, k == 7,
                           wkeys(so) + MQ, [PK(po)])
                    STT(xT[:, c, t0:t0 + NT], ps[po][:, :NT], modT[:, l, r, 16 + c:17 + c], xT[:, c, t0:t0 + NT],
                        ALU.mult, ALU.add, [PK(po), "modT", XK(c, bidx)], [XK(c, bidx)])

        stage = tmp[:, 0:2, :]
        for bi in range(nb):
            set_phase('load')
            for gt in range(18):
                src = ctx2[bi, gt * 128:(gt + 1) * 128, :] if gt < 2 else x2[bi, (gt - 2) * 128:(gt - 1) * 128, :]
                SPDMA(stage.rearrange("p a b -> p (a b)"), src, [], [("tmp", 0), ("tmp", 1)])
                bidx = 0 if gt < 2 else 1 + (gt - 2) // 4
                for hf in range(2):
                    pb = bank("proj")
                    for cq in range(4):
                        c = hf * 4 + cq
                        TR(ps[pb][:, cq * 128:(cq + 1) * 128], tmp[:, c // 4, (c % 4) * 128:(c % 4) * 128 + 128],
                           [("tmp", 0), ("tmp", 1)], [PK(pb)])
                    CPY("dve" if hf == 0 else "act", xT[:, hf * 4:hf * 4 + 4, gt * 128:(gt + 1) * 128],
                        ps[pb][:, :].rearrange("p (a b) -> p a b", a=4), [PK(pb)], [XK(c, bidx) for c in range(hf * 4, hf * 4 + 4)])
            for l in range(nl):
                last = (l == nl - 1)
                for blk in blocks:
                    phase1(l, bi, blk)
                for blk in blocks:
                    if blk[3] and last:
                        continue
                    phase2(l, bi, blk, last)
                if dbg and bi == 0 and l == 0:
                    for name, t in (("kaT", kaT), ("kcT", kcT), ("va", va), ("vc", vc), ("uT", uT), ("yg", yg), ("mq", mq),
                                    ("xT", xT), ("modT", modT), ("nlam", nlam), ("prmT", prmT)):
                        dd = dt("dbg_" + name, list(t.shape), t.dtype, "ExternalOutput")
                        dbg_t[name] = dd
                        allk = list(P.lastw.keys())
                        SPDMA(dd, t[:], allk, [])
            set_phase('final')
            for j in range(4):
                bidx, t0, NT = 1 + j, L + 512 * j, 512
                for c in range(8):
                    ACT(SQ(c % 2)[:, :NT], xT[:, c, t0:t0 + NT], AF.Square, [XK(c, bidx)], [SQK(c % 2)])
                    MM(ps[7][:, :NT], onesM, SQ(c % 2)[:, :NT], c == 0, c == 7, [SQK(c % 2)] + CB, [PK(7)], signal=True)
                ACT(tmp[:, 5, :NT], ps[7][:, :NT], AF.Sqrt, [PK(7), "epsT"], [("tmp", 5)], bias=epsT[:, 0:1])
                RECIP(tmp[:, 5, :NT], tmp[:, 5, :NT], [("tmp", 5)], [("tmp", 5)])
                for tt in range(4):
                    ytile = tmp[:, 2:4, :].rearrange("p a b -> p (a b)")
                    for c in range(8):
                        STT(ytile[:, c * 128:(c + 1) * 128], xT[:, c, t0 + tt * 128:t0 + (tt + 1) * 128],
                            prmT[:, R_FN + c:R_FN + c + 1], tmp[:, 5, tt * 128:(tt + 1) * 128], ALU.mult, ALU.mult,
                            [XK(c, bidx), "prmT", ("tmp", 5)], [("tmp", 2), ("tmp", 3)])
                    for hf in range(2):
                        pb = bank("proj")
                        for cq in range(4):
                            c = hf * 4 + cq
                            TR(ps[pb][:, cq * 128:(cq + 1) * 128], ytile[:, c * 128:(c + 1) * 128],
                               [("tmp", 2), ("tmp", 3)], [PK(pb)])
                        CPY("dve" if hf == 0 else "act", tmp[:, hf, :], ps[pb][:, :], [PK(pb)], [("tmp", hf)])
                    SPDMA(out2[bi, (t0 - L) + tt * 128:(t0 - L) + (tt + 1) * 128, :], stage.rearrange("p a b -> p (a b)"),
                          [("tmp", 0), ("tmp", 1)], [])
        P.wait_all("sp", list(P.lastw.keys()) + list(P.readers.keys()))
        P.run()
    dbg_t['phase_log'] = phase_log
    return nc, dbg_t


def rope_tables():
    inv_freq = (np.float32(10000.0) ** (-np.arange(16, dtype=np.float32) / np.float32(16))).astype(np.float32)
    t = np.arange(S)
    rows = (t // 64).astype(np.float32)
    cols = (t % 64).astype(np.float32)
    p = np.arange(128)
    d = p % 64
    f = d % 16
    axis = d // 32
    pos = np.where(axis[:, None] == 0, rows[None, :], cols[None, :]).astype(np.float32)
    ang = (pos * inv_freq[f][:, None]).astype(np.float32)
    return np.cos(ang).astype(np.float32), np.sin(ang).astype(np.float32)


def const_mats():
    c = np.zeros((6, 128, 128), np.float32)
    c[0] = np.eye(128, dtype=np.float32)
    for m in range(128):
        if m % 32 < 16:
            c[1][m + 16, m] = -1.0
        else:
            c[1][m - 16, m] = 1.0
    c[2][:] = 1.0 / 1024.0
    c[3][0:64, 0:64] = 1.0 / 64.0
    c[3][64:128, 64:128] = 1.0 / 64.0
    c[4][:] = 1.0 / 128.0
    c[5][:] = 1.0
    return c


def make_prm(inp, b0, b1, nl):
    prm = np.zeros((R_TOT, 128), np.float32)
    prm[0:8] = np.asarray(inp["c"][b0]).reshape(8, 128)
    if b1 is not None:
        prm[8:16] = np.asarray(inp["c"][b1]).reshape(8, 128)
    prm[16:24] = np.asarray(inp["c_ctx"]).reshape(8, 128)
    for l in range(nl):
        prm[R_BMOD + l * 24:R_BMOD + (l + 1) * 24] = np.asarray(inp["b_mod"][l]).reshape(24, 128)
        prm[R_CONV + l * 12:R_CONV + (l + 1) * 12] = np.asarray(inp["conv_w_b"][l]).reshape(12, 128)
        prm[R_SUBLN + l] = np.asarray(inp["subln_c"][l])
        prm[R_QN + l] = np.tile(np.asarray(inp["q_norm_a"][l]), 2)
        prm[R_KN + l] = np.tile(np.asarray(inp["k_norm_a"][l]), 2)
        for i, nm in enumerate(("lam_q1", "lam_k1", "lam_q2", "lam_k2")):
            prm[R_LAM + 4 * i + l, 0:64] = np.asarray(inp[nm][l])
    prm[R_FN:R_FN + 8] = np.asarray(inp["final_norm"]).reshape(8, 128)
    return prm


def make_in_maps(inp, ncores, nb, nl):
    rc, rs = rope_tables()
    cst = const_mats()
    f = lambda a: np.ascontiguousarray(np.asarray(a, dtype=np.float32))
    shared = dict(cst=cst, ropec=rc, ropes=rs, w_mod=f(inp["w_mod"][:nl]), w_in=f(inp["w_in"][:nl]),
                  w_a=f(inp["w_branch_a"][:nl]), w_b=f(inp["w_branch_b"][:nl]), w_c=f(inp["w_branch_c"][:nl]),
                  w_o=f(inp["w_out"][:nl]))
    maps = []
    for i in range(ncores):
        b0 = i * nb
        b1 = b0 + 1 if nb > 1 else None
        m = dict(shared)
        m["x2"] = f(inp["x"][b0:b0 + nb])
        m["ctx2"] = f(inp["ctx"][b0:b0 + nb])
        m["prm"] = make_prm(inp, b0, b1, nl)
        maps.append(m)
    return maps


def kernel(**inputs):
    nc, _ = build_program(DEPTH, 2)
    maps = make_in_maps(inputs, NCORES, 2, DEPTH)
    res = run_bass_kernel_spmd(nc, maps, core_ids=list(range(NCORES)))
    out = np.concatenate([np.asarray(r["out2"]) for r in res.results], axis=0)
    return out.astype(np.float32)
```
